# Optimizing a Trainium2 kernel written in Bass

```python
import jax, jax.numpy as jnp
from jax import lax
import numpy as np

D_MODEL = 1024
BATCH = 1
SEQ = 16384
DEPTH = 2
DEC_BATCH = 8
DEC_SEQ = 4096
PAST_LEN = 128

N_MEM = 256
D_CONV_A = D_MODEL
CONV_A_WIDTH = 31
D_CONV_B = D_MODEL
CONV_B_WIDTH = 3
N_XHEADS = 4
D_XATTN = D_MODEL
XHEAD_DIM = D_XATTN // N_XHEADS
N_BRANCH = 3
D_FF = 2816
EPS = 1e-6
D_IN = 2 * D_CONV_A + 3 * D_CONV_B + D_XATTN + N_BRANCH * D_MODEL

kernel_name = "macaron_parallel_conv_xattn_encoder"


def rmsnorm(x, g):
    xf = x.astype(jnp.float32)
    y = xf * lax.rsqrt(jnp.mean(xf * xf, axis=-1, keepdims=True) + EPS)
    return (y * g.astype(jnp.float32)).astype(x.dtype)


def layernorm(x, g, b):
    xf = x.astype(jnp.float32)
    mu = jnp.mean(xf, axis=-1, keepdims=True)
    xc = xf - mu
    var = jnp.mean(xc * xc, axis=-1, keepdims=True)
    y = xc * lax.rsqrt(var + EPS) * g.astype(jnp.float32) + b.astype(jnp.float32)
    return y.astype(x.dtype)


def swiglu(x, w_gate, w_up, w_down):
    return (jax.nn.silu(x @ w_gate) * (x @ w_up)) @ w_down


def depthwise_conv(x, w):
    pad = w.shape[0] // 2
    return lax.conv_general_dilated(
        x, w[:, None, :], window_strides=(1,), padding=[(pad, pad)],
        dimension_numbers=("NWC", "WIO", "NWC"), feature_group_count=x.shape[-1])


def cross_attention(q, mem_n, w_kv, w_out):
    b, s, _ = q.shape
    m = mem_n.shape[1]
    k, v = jnp.split(mem_n @ w_kv, 2, axis=-1)
    q = q.reshape(b, s, N_XHEADS, XHEAD_DIM)
    k = k.reshape(b, m, N_XHEADS, XHEAD_DIM)
    v = v.reshape(b, m, N_XHEADS, XHEAD_DIM)
    scores = jnp.einsum("bshd,bmhd->bhsm", q, k).astype(jnp.float32) * (XHEAD_DIM ** -0.5)
    probs = jax.nn.softmax(scores, axis=-1).astype(v.dtype)
    o = jnp.einsum("bhsm,bmhd->bshd", probs, v).reshape(b, s, D_XATTN)
    return o @ w_out


def encoder_layer(x, mem, p):
    (ffn1_norm, ffn1_wg, ffn1_wu, ffn1_wd, mix_norm, mem_norm, w_in,
     conv_a_w, conv_a_b, ln_a_g, ln_a_b, w_a_out, conv_b_w, w_b_out,
     w_kv, w_x_out, w_o, ffn2_norm, ffn2_wg, ffn2_wu, ffn2_wd) = p
    b, s, _ = x.shape
    x = x + 0.5 * swiglu(rmsnorm(x, ffn1_norm), ffn1_wg, ffn1_wu, ffn1_wd)
    u = rmsnorm(x, mix_norm)
    z = u @ w_in
    splits = [D_CONV_A, 2 * D_CONV_A, 2 * D_CONV_A + D_CONV_B, 2 * D_CONV_A + 2 * D_CONV_B,
              2 * D_CONV_A + 3 * D_CONV_B, 2 * D_CONV_A + 3 * D_CONV_B + D_XATTN]
    a_val, a_gate, b_x, b_gate_b, b_gate_c, q, gates = jnp.split(z, splits, axis=-1)
    a = a_val * jax.nn.sigmoid(a_gate)
    a = depthwise_conv(a, conv_a_w) + conv_a_b
    a = jax.nn.silu(layernorm(a, ln_a_g, ln_a_b))
    y_a = a @ w_a_out
    y_b = (b_gate_b * depthwise_conv(b_gate_c * b_x, conv_b_w)) @ w_b_out
    y_c = cross_attention(q, rmsnorm(mem, mem_norm), w_kv, w_x_out)
    g = jax.nn.sigmoid(gates).reshape(b, s, N_BRANCH, D_MODEL)
    merged = g[:, :, 0, :] * y_a + g[:, :, 1, :] * y_b + g[:, :, 2, :] * y_c
    x = x + merged @ w_o
    x = x + 0.5 * swiglu(rmsnorm(x, ffn2_norm), ffn2_wg, ffn2_wu, ffn2_wd)
    return x


def encoder_trunk(x, mem, layer_params, final_norm):
    for l in range(DEPTH):
        x = encoder_layer(x, mem, tuple(w[l] for w in layer_params))
    return rmsnorm(x, final_norm)


def setup_inputs(seed: int = 0) -> dict:
    key = jax.random.key(seed)
    ks = iter(jax.random.split(key, 32))

    def dense(shape, fan_in):
        return jax.random.normal(next(ks), shape, jnp.float32) * (fan_in ** -0.5)

    def gain(shape):
        return 1.0 + 0.01 * jax.random.normal(next(ks), shape, jnp.float32)

    def small(shape):
        return 0.01 * jax.random.normal(next(ks), shape, jnp.float32)

    L, D = DEPTH, D_MODEL
    return {
        "x_prompt": jax.random.normal(next(ks), (BATCH, SEQ, D), jnp.float32),
        "x_sample": jax.random.normal(next(ks), (DEC_BATCH, DEC_SEQ, D), jnp.float32),
        "mem_prompt": jax.random.normal(next(ks), (BATCH, N_MEM, D), jnp.float32),
        "mem_sample": jax.random.normal(next(ks), (DEC_BATCH, N_MEM, D), jnp.float32),
        "ffn1_norm": gain((L, D)),
        "ffn1_wg": dense((L, D, D_FF), D),
        "ffn1_wu": dense((L, D, D_FF), D),
        "ffn1_wd": dense((L, D_FF, D), D_FF),
        "mix_norm": gain((L, D)),
        "mem_norm": gain((L, D)),
        "w_in": dense((L, D, D_IN), D),
        "conv_a_w": dense((L, CONV_A_WIDTH, D_CONV_A), CONV_A_WIDTH),
        "conv_a_b": small((L, D_CONV_A)),
        "ln_a_g": gain((L, D_CONV_A)),
        "ln_a_b": small((L, D_CONV_A)),
        "w_a_out": dense((L, D_CONV_A, D), D_CONV_A),
        "conv_b_w": dense((L, CONV_B_WIDTH, D_CONV_B), CONV_B_WIDTH),
        "w_b_out": dense((L, D_CONV_B, D), D_CONV_B),
        "w_kv": dense((L, D, 2 * D_XATTN), D),
        "w_x_out": dense((L, D_XATTN, D), D_XATTN),
        "w_o": dense((L, D, D), D),
        "ffn2_norm": gain((L, D)),
        "ffn2_wg": dense((L, D, D_FF), D),
        "ffn2_wu": dense((L, D, D_FF), D),
        "ffn2_wd": dense((L, D_FF, D), D_FF),
        "final_norm": gain((D,)),
    }


def reference(x_prompt, x_sample, mem_prompt, mem_sample,
              ffn1_norm, ffn1_wg, ffn1_wu, ffn1_wd, mix_norm, mem_norm, w_in,
              conv_a_w, conv_a_b, ln_a_g, ln_a_b, w_a_out, conv_b_w, w_b_out,
              w_kv, w_x_out, w_o, ffn2_norm, ffn2_wg, ffn2_wu, ffn2_wd, final_norm):
    layer_params = (ffn1_norm, ffn1_wg, ffn1_wu, ffn1_wd, mix_norm, mem_norm, w_in,
                    conv_a_w, conv_a_b, ln_a_g, ln_a_b, w_a_out, conv_b_w, w_b_out,
                    w_kv, w_x_out, w_o, ffn2_norm, ffn2_wg, ffn2_wu, ffn2_wd)
    y_prompt = encoder_trunk(x_prompt, mem_prompt, layer_params, final_norm)
    y_sample = encoder_trunk(x_sample, mem_sample, layer_params, final_norm)
    return (y_prompt, y_sample)
```

```python
import contextlib
import numpy as np
import concourse.bass as bass
import concourse.mybir as mybir
from concourse.bass_utils import run_bass_kernel_spmd

F32 = mybir.dt.float32
BF16 = mybir.dt.bfloat16
AF = mybir.ActivationFunctionType
ALU = mybir.AluOpType

D = 1024
NCH = 8
DFF = 2816
NJ = 22
L = 2
DIN = 9216
NMEM = 256
HALO = 30
TOUT = 1024
W = TOUT + 2 * HALO
SUBS = [(0, 512), (512, 512), (1024, 60)]
NSUB = len(SUBS)
NT = 6
PAD = 15
WP = W + 2 * PAD
RING = 4
NTD = 7
SLAB = 4096
EPS = 1e-6
ALLC = list(range(8))
ALLS = list(range(NSUB))
SQK = [("sq", c) for c in range(8)]

C_FFN1N, C_MIXN, C_MEMN, C_FFN2N, C_CAB, C_LNG, C_LNB, C_CAW, C_CBW = 0, 8, 16, 24, 32, 40, 48, 56, 304
PER = 328
C_FINAL = 2 * PER
C_EPS = C_FINAL + 8
NC_ = 672

SAME_ENGINE_SYNC = False
ENGS = ("pe", "act", "dve", "pool", "sp")

WNAMES = [("ffn1_wg", D, DFF), ("ffn1_wu", D, DFF), ("ffn1_wd", DFF, D), ("w_in", D, DIN),
          ("w_a_out", D, D), ("w_b_out", D, D), ("w_kv", D, 2 * D), ("w_x_out", D, D), ("w_o", D, D),
          ("ffn2_wg", D, DFF), ("ffn2_wu", D, DFF), ("ffn2_wd", DFF, D)]


class Prog:
    def __init__(self):
        self.ops = {e: [] for e in ENGS}
        self.cnt = {}
        self.lastw = {}
        self.readers = {}
        self.waited = {e: {} for e in ENGS}
        self.pending = {e: {} for e in ENGS}
        self.nops = 0
        self.stamp = {}

    def op(self, eng, fn, reads=(), writes=(), dma=None):
        self.nops += 1
        for r in reads:
            if r[0] == "ps":
                self.stamp[r[1]] = self.nops
        for r in writes:
            if r[0] == "ps":
                self.stamp[r[1]] = self.nops
        semname = dma if dma else eng
        inc = 16 if dma else 1
        self.cnt[semname] = self.cnt.get(semname, 0) + inc
        me = (semname, self.cnt[semname])
        deps = dict(self.pending[eng])
        self.pending[eng] = {}

        def add(d):
            if d is None:
                return
            s, c = d
            if s == eng and (eng == "pe" or not SAME_ENGINE_SYNC):
                return
            if deps.get(s, 0) < c:
                deps[s] = c

        for r in reads:
            add(self.lastw.get(r))
        for r in writes:
            add(self.lastw.get(r))
            rd = self.readers.get(r)
            if rd:
                for s, c in rd.items():
                    add((s, c))
        waits = []
        wd = self.waited[eng]
        for s, c in deps.items():
            if wd.get(s, 0) < c:
                wd[s] = c
                waits.append((s, c))
        for r in reads:
            self.readers.setdefault(r, {})[me[0]] = me[1]
        for r in writes:
            self.lastw[r] = me
            self.readers[r] = {}
        self.ops[eng].append((waits, fn, semname, inc))

    def barrier(self):
        names = [n for n in self.cnt if not n.startswith(("ring", "xld", "st", "dgld"))]
        for e in ("pe", "act", "dve", "sp"):
            for n in names:
                if n == e:
                    continue
                c = self.cnt[n]
                if self.pending[e].get(n, 0) < c:
                    self.pending[e][n] = c


def rs(name, cs, ss):
    if isinstance(cs, int):
        cs = [cs]
    if isinstance(ss, int):
        ss = [ss]
    return [(name, c, s) for c in cs for s in ss]


def build_program(groups):
    nc = bass.Bass("TRN2", target_bir_lowering=False)
    P = Prog()
    dr = {}
    for name, a, b in WNAMES:
        dr[name] = nc.dram_tensor(name, [L, a, b], F32, kind="ExternalInput").ap()
    xt_d = nc.dram_tensor("xt", [NT, 128 * 8 * W], F32, kind="ExternalInput").ap()
    mem_d = nc.dram_tensor("memt", [2, D, NMEM], F32, kind="ExternalInput").ap()
    cst_d = nc.dram_tensor("consts", [128, NC_], F32, kind="ExternalInput").ap()
    msk_d = nc.dram_tensor("masks", [128, NT * 60], F32, kind="ExternalInput").ap()
    idn_d = nc.dram_tensor("ident", [128, 128], F32, kind="ExternalInput").ap()
    yt_d = nc.dram_tensor("yt", [NT, 128, 8 * TOUT], F32, kind="ExternalOutput").ap()
    dg_d = nc.dram_tensor("dgscr", [L, 8, 128, 31 * 128], BF16, kind="Internal").ap()

    es = contextlib.ExitStack()

    def T(name, shape, dt):
        return es.enter_context(nc.sbuf_tensor(name, shape, dt))

    with es:
        x32 = T("x32", [128, 8, W], F32)
        u16 = T("u16", [128, 8, W], BF16)
        SCR = 8 * WP + 16 * W
        scr = T("scr", [128, SCR], BF16)
        ring = T("ring", [128, RING, SLAB], BF16)
        KT = T("KT", [128, L, 8, NMEM], BF16)
        VV = T("VV", [128, L, 2, D], BF16)
        diag = T("diag", [128, 2, 31, 128], BF16)
        at = T("at", [128, 3, 512], F32)
        dtm = T("dtm", [128, 2, 512], F32)
        stat = T("stat", [128, 4, 512], F32)
        stat2 = T("stat2", [128, 2, 64], F32)
        pT = T("pT", [128, 2, 2, 512], BF16)
        sq = T("sq", [128, 8, 512], BF16)
        cst = T("cst", [128, NC_], F32)
        maskt = T("maskt", [128, NT * 60], F32)
        id32 = T("id32", [128, 128], F32)
        ident = T("identb", [128, 128], BF16)
        ones_m = T("ones_m", [128, 128], BF16)
        ones1 = T("ones1", [128, 128], BF16)
        ps = [es.enter_context(nc.psum_tensor(f"ps{b}", [128, 512], F32)) for b in range(8)]

        R1 = scr[:, 0:8 * WP].rearrange("p (c w) -> p c w", c=8)
        R2 = scr[:, 8 * WP:8 * WP + 8 * W].rearrange("p (c w) -> p c w", c=8)
        R3 = scr[:, 8 * WP + 8 * W:8 * WP + 16 * W].rearrange("p (c w) -> p c w", c=8)
        hbuf = scr[:, 0:NJ * W].rearrange("p (c w) -> p c w", c=NJ)
        ostage = scr[:, 0:2 * 8 * TOUT].bitcast(F32).rearrange("p (c w) -> p c w", c=8)
        mem32 = scr[:, 0:2 * 8 * NMEM].bitcast(F32).rearrange("p (c w) -> p c w", c=8)
        memn = scr[:, 2 * 8 * NMEM:3 * 8 * NMEM].rearrange("p (c w) -> p c w", c=8)

        state = {"bank": 0, "slab": 0}
        rot = {}

        def nexti(name, n):
            i = rot.get(name, 0)
            rot[name] = (i + 1) % n
            return i

        reserved = set()

        def bank():
            free = [b for b in range(8) if b not in reserved]
            b = min(free, key=lambda k: (P.stamp.get(k, 0), k))
            P.stamp[b] = P.nops + 1
            return b

        def stats_begin():
            sb = [bank() for _ in range(NSUB)]
            reserved.update(sb)
            state["statb"] = sb

        def x_updated(c, si):
            sb = state.get("statb")
            if sb is None:
                return
            s0, n = SUBS[si]
            b = sb[si]
            q = nexti("sqslot", 8)
            act(sq[:, q, 0:n], x32[:, c, s0:s0 + n], AF.Square, reads=[("x", c, si)], writes=[("sq", q)])
            pend = state.setdefault("pend", [])
            pend.append(lambda: P.op(
                "pe", lambda t: t.matmul(ps[b][:, 0:n], ones_m[:], sq[:, q, 0:n], start=(c == 0), stop=(c == 7)),
                reads=[("sq", q), ("one",)], writes=[("ps", b)]))
            nn = state.get("next_norm")
            early = (c == 7 and nn is not None)
            while len(pend) > (0 if early else 4):
                pend.pop(0)()
            if early:
                norm_si(si, b, nn[0], nn[1])
                state.setdefault("norm_done", set()).add(si)

        def mm(b, n, pairs, reads, usplit=None):
            np_ = len(pairs)
            if usplit is not None and state.pop("split_next", False):
                for i, (l, r) in enumerate(pairs):
                    P.op("pe", lambda t, l=l, r=r, i=i: t.matmul(ps[b][:, 0:n], l, r, start=(i == 0),
                                                               stop=(i == np_ - 1)),
                         reads=[x for x in reads if x[0] != "u"] + [("u", i, usplit)], writes=[("ps", b)])
                return

            def fn(t):
                last = None
                for i, (l, r) in enumerate(pairs):
                    last = t.matmul(ps[b][:, 0:n], l, r, start=(i == 0), stop=(i == np_ - 1))
                return last
            P.op("pe", fn, reads=reads, writes=[("ps", b)])

        def act(out, in_, func, reads, writes, bias=None, scale=None):
            kw = {}
            if bias is not None:
                kw["bias"] = bias
            if scale is not None:
                kw["scale"] = scale
            P.op("act", lambda a: a.activation(out=out, in_=in_, func=func, **kw), reads=reads, writes=writes)

        def tt(out, in0, in1, op, reads, writes):
            P.op("dve", lambda v: v.tensor_tensor(out=out, in0=in0, in1=in1, op=op), reads=reads, writes=writes)

        def stt(out, in0, scalar, in1, op0, op1, reads, writes):
            P.op("dve", lambda v: v.scalar_tensor_tensor(out=out, in0=in0, scalar=scalar, in1=in1, op0=op0, op1=op1),
                 reads=reads, writes=writes)

        def memset(ap, val, writes):
            P.op("dve", lambda v: v.memset(ap, val), reads=[], writes=writes)

        def slab(src):
            i = state["slab"]
            state["slab"] += 1
            slot = i % RING
            _, a, b = src.shape
            assert a * b <= SLAB
            dst = ring[:, slot, 0:a * b].rearrange("p (a b) -> p a b", a=a)
            P.op("pool", lambda g: g.dma_start(out=dst, in_=src), reads=[], writes=[("slot", slot)], dma=f"ring{slot}")
            return dst, ("slot", slot)

        def kview(name, l):
            return dr[name][l].rearrange("(k p) n -> p k n", p=128)

        def cc(col):
            return cst[:, col:col + 1]

        def norm_si(si, b, goff, final):
            s0, n = SUBS[si]
            if n <= 64:
                sbuf_, skey, k0 = stat2, "stat2", 0
            else:
                sbuf_, skey, k0 = stat, "stat", (si % 2) * 2
            act(sbuf_[:, k0, 0:n], ps[b][:, 0:n], AF.Ln, bias=cc(C_EPS), reads=[("ps", b), ("cst",)],
                writes=[(skey, k0)])
            act(sbuf_[:, k0 + 1, 0:n], sbuf_[:, k0, 0:n], AF.Exp, scale=-0.5, reads=[(skey, k0)],
                writes=[(skey, k0 + 1)])
            for c in range(8):
                if not final:
                    stt(u16[:, c, s0:s0 + n], x32[:, c, s0:s0 + n], cc(goff + c), sbuf_[:, k0 + 1, 0:n],
                        ALU.mult, ALU.mult, reads=[("x", c, si), (skey, k0 + 1), ("cst",)],
                        writes=[("u", c, si)])
                else:
                    lo = max(s0, HALO)
                    hi = min(s0 + n, HALO + TOUT)
                    stt(ostage[:, c, lo - HALO:hi - HALO], x32[:, c, lo:hi], cc(goff + c),
                        sbuf_[:, k0 + 1, lo - s0:hi - s0], ALU.mult, ALU.mult,
                        reads=[("x", c, si), (skey, k0 + 1), ("cst",)], writes=[("ost",)])

        def rmsnorm(goff, final=False):
            sb = state.pop("statb", None)
            if sb is not None:
                reserved.difference_update(sb)
                for f in state.pop("pend", []):
                    f()
            done = state.pop("norm_done", set())
            state.pop("next_norm", None)
            for si, (s0, n) in enumerate(SUBS):
                if si in done:
                    continue
                if sb is not None:
                    b = sb[si]
                else:
                    act(sq[:, :, 0:n], x32[:, :, s0:s0 + n], AF.Square, reads=rs("x", ALLC, si), writes=SQK)
                    b = bank()
                    mm(b, n, [(ones_m[:], sq[:, c, 0:n]) for c in range(8)], reads=SQK + [("one",)])
                norm_si(si, b, goff, final)

        def pair_stage(*a, **kw):
            for _ in pair_stage_gen(*a, **kw):
                pass

        def pair_stage_gen(l, pname, pcol, qname, qcol, func, dst, dname, doff, halves=(0, 1, 2, 3)):
            for half in halves:
                pw, pres = slab(kview(pname, l)[:, :, pcol + half * 256:pcol + (half + 1) * 256])
                qw, qres = slab(kview(qname, l)[:, :, qcol + half * 256:qcol + (half + 1) * 256])
                for m in range(2):
                    c = half * 2 + m
                    for si, (s0, n) in enumerate(SUBS):
                        bp = bank()
                        mm(bp, n, [(pw[:, k, m * 128:(m + 1) * 128], u16[:, k, s0:s0 + n]) for k in range(8)],
                           reads=[pres] + rs("u", ALLC, si), usplit=si)
                        bq = bank()
                        mm(bq, n, [(qw[:, k, m * 128:(m + 1) * 128], u16[:, k, s0:s0 + n]) for k in range(8)],
                           reads=[qres] + rs("u", ALLC, si))
                        ai = nexti("at", 3)
                        act(at[:, ai, 0:n], ps[bq][:, 0:n], func, reads=[("ps", bq)], writes=[("at", ai)])
                        tt(dst[:, c, doff + s0:doff + s0 + n], ps[bp][:, 0:n], at[:, ai, 0:n], ALU.mult,
                           reads=[("ps", bp), ("at", ai)], writes=[(dname, c, si)])
                        yield

        def ffn(l, pre, goff):
            P.barrier()
            rmsnorm(goff)
            state["split_next"] = True
            wg, wu, wd = pre + "_wg", pre + "_wu", pre + "_wd"
            for jj in range(NJ // 2):
                gw, gres = slab(kview(wg, l)[:, :, jj * 256:(jj + 1) * 256])
                uw, ures = slab(kview(wu, l)[:, :, jj * 256:(jj + 1) * 256])
                for m in range(2):
                    j = jj * 2 + m
                    for si, (s0, n) in enumerate(SUBS):
                        bg = bank()
                        mm(bg, n, [(gw[:, k, m * 128:(m + 1) * 128], u16[:, k, s0:s0 + n]) for k in range(8)],
                           reads=[gres] + rs("u", ALLC, si), usplit=si)
                        bu = bank()
                        mm(bu, n, [(uw[:, k, m * 128:(m + 1) * 128], u16[:, k, s0:s0 + n]) for k in range(8)],
                           reads=[ures] + rs("u", ALLC, si))
                        ai = nexti("at", 3)
                        act(at[:, ai, 0:n], ps[bg][:, 0:n], AF.Silu, reads=[("ps", bg)], writes=[("at", ai)])
                        tt(hbuf[:, j, s0:s0 + n], ps[bu][:, 0:n], at[:, ai, 0:n], ALU.mult,
                           reads=[("ps", bu), ("at", ai)], writes=[("h", j, si), ("ost",)])
            wdv = dr[wd][l].rearrange("(j p) n -> p j n", p=128)
            if pre == "ffn1":
                state["next_norm"] = (l * PER + C_MIXN, False)
            elif l + 1 < L:
                state["next_norm"] = ((l + 1) * PER + C_FFN1N, False)
            stats_begin()
            for m in range(8):
                dw, dres = slab(wdv[:, :, m * 128:(m + 1) * 128])
                for si, (s0, n) in enumerate(SUBS):
                    b = bank()
                    mm(b, n, [(dw[:, j, :], hbuf[:, j, s0:s0 + n]) for j in range(NJ)],
                       reads=[dres] + rs("h", list(range(NJ)), si))
                    stt(x32[:, m, s0:s0 + n], ps[b][:, 0:n], 0.5, x32[:, m, s0:s0 + n], ALU.mult, ALU.add,
                        reads=[("ps", b), ("x", m, si)], writes=[("x", m, si)])
                    x_updated(m, si)

        def zero_pads():
            memset(R1[:, :, 0:PAD], 0.0, writes=[("r1pad",)])
            memset(R1[:, :, PAD + W:WP], 0.0, writes=[("r1pad",)])

        def mask_edges(ti):
            ml = maskt[:, ti * 60:ti * 60 + 30].unsqueeze(1).to_broadcast([128, 8, 30])
            mr = maskt[:, ti * 60 + 30:ti * 60 + 60].unsqueeze(1).to_broadcast([128, 8, 30])
            tt(R1[:, :, PAD:PAD + 30], R1[:, :, PAD:PAD + 30], ml, ALU.mult,
               reads=rs("r1", ALLC, 0) + [("cst",)], writes=rs("r1", ALLC, 0))
            tt(R1[:, :, PAD + W - 30:PAD + W], R1[:, :, PAD + W - 30:PAD + W], mr, ALU.mult,
               reads=rs("r1", ALLC, 2) + [("cst",)], writes=rs("r1", ALLC, 2))

        def proj_gate(l, wname, gi, first):
            gcol = 6 * D + gi * D
            for half in range(4):
                ww, wres = slab(kview(wname, l)[:, :, half * 256:(half + 1) * 256])
                gw, gres = slab(kview("w_in", l)[:, :, gcol + half * 256:gcol + (half + 1) * 256])
                for m in range(2):
                    c = half * 2 + m
                    for si, (s0, n) in enumerate(SUBS):
                        by = bank()
                        mm(by, n, [(ww[:, k, m * 128:(m + 1) * 128], R2[:, k, s0:s0 + n]) for k in range(8)],
                           reads=[wres] + rs("r2", ALLC, si))
                        bg = bank()
                        mm(bg, n, [(gw[:, k, m * 128:(m + 1) * 128], u16[:, k, s0:s0 + n]) for k in range(8)],
                           reads=[gres] + rs("u", ALLC, si))
                        ai = nexti("at", 3)
                        act(at[:, ai, 0:n], ps[bg][:, 0:n], AF.Sigmoid, reads=[("ps", bg)], writes=[("at", ai)])
                        if first:
                            tt(R3[:, c, s0:s0 + n], ps[by][:, 0:n], at[:, ai, 0:n], ALU.mult,
                               reads=[("ps", by), ("at", ai)], writes=[("r3", c, si)])
                        else:
                            di = nexti("dt", 2)
                            tt(dtm[:, di, 0:n], ps[by][:, 0:n], at[:, ai, 0:n], ALU.mult,
                               reads=[("ps", by), ("at", ai)], writes=[("dt", di)])
                            tt(R3[:, c, s0:s0 + n], R3[:, c, s0:s0 + n], dtm[:, di, 0:n], ALU.add,
                               reads=[("r3", c, si), ("dt", di)], writes=[("r3", c, si)])

        def build_diag(di, col0, ntap):
            i0 = ident[:].unsqueeze(1).to_broadcast([128, ntap, 128])
            i1 = cst[:, col0:col0 + ntap].unsqueeze(2).to_broadcast([128, ntap, 128])
            tt(diag[:, di, 0:ntap, :], i0, i1, ALU.mult, reads=[("cst",), ("idn",)], writes=[("diag", di)])

        def load_diag(l, c):
            di = c % 2
            P.op("sp", lambda s_: s_.dma_start(out=diag[:, di, :, :].rearrange("p k j -> p (k j)"), in_=dg_d[l, c]),
                 reads=[("dgd", 0), ("dgd", 1)], writes=[("diag", di)], dma=f"dgld{di}")

        def mixer(l, ti):
            base = l * PER
            P.barrier()
            rmsnorm(base + C_MIXN)
            state["split_next"] = True
            zero_pads()
            pair_stage(l, "w_in", 0, "w_in", D, AF.Sigmoid, R1, "r1", PAD)
            mask_edges(ti)
            dis = [c % 2 for c in range(8)]
            load_diag(l, 0)
            for c in range(8):
                di = dis[c]
                if c + 1 < 8:
                    load_diag(l, c + 1)
                for si, (s0, n) in enumerate(SUBS):
                    b = bank()
                    mm(b, n, [(diag[:, di, k, :], R1[:, c, s0 + k:s0 + k + n]) for k in range(NTD, 31)],
                       reads=[("diag", di), ("r1pad",)] + rs("r1", c, ALLS))
                    ci = nexti("dt", 2)
                    act(dtm[:, ci, 0:n], ps[b][:, 0:n], AF.Identity, bias=cc(base + C_CAB + c),
                        reads=[("ps", b), ("cst",)], writes=[("dt", ci)])
                    for k in range(NTD):
                        last = (k == NTD - 1)
                        stt(R3[:, c, s0:s0 + n] if last else dtm[:, ci, 0:n], R1[:, c, s0 + k:s0 + k + n],
                            cc(base + C_CAW + c * 31 + k), dtm[:, ci, 0:n], ALU.mult, ALU.add,
                            reads=[("dt", ci), ("cst",), ("r1pad",)] + rs("r1", c, ALLS),
                            writes=[("r3", c, si)] if last else [("dt", ci)])
            def ln_part1(si):
                s0, n = SUBS[si]
                act(sq[:, :, 0:n], R3[:, :, s0:s0 + n], AF.Square, reads=rs("r3", ALLC, si), writes=SQK)

            def ln_part2_gen(si):
                s0, n = SUBS[si]
                bm = bank()
                reserved.add(bm)
                mm(bm, n, [(ones_m[:], R3[:, c, s0:s0 + n]) for c in range(8)], reads=rs("r3", ALLC, si) + [("one",)])
                bq = bank()
                reserved.add(bq)
                mm(bq, n, [(ones_m[:], sq[:, c, 0:n]) for c in range(8)], reads=SQK + [("one",)])
                act(stat[:, 0, 0:n], ps[bm][:, 0:n], AF.Square, reads=[("ps", bm)], writes=[("stat", 0)])
                tt(stat[:, 1, 0:n], ps[bq][:, 0:n], stat[:, 0, 0:n], ALU.subtract,
                   reads=[("ps", bq), ("stat", 0)], writes=[("stat", 1)])
                act(stat[:, 2, 0:n], stat[:, 1, 0:n], AF.Ln, bias=cc(C_EPS), reads=[("stat", 1), ("cst",)],
                    writes=[("stat", 2)])
                act(stat[:, 3, 0:n], stat[:, 2, 0:n], AF.Exp, scale=-0.5, reads=[("stat", 2)], writes=[("stat", 3)])
                reserved.discard(bq)
                yield
                for c in range(8):
                    di = nexti("dt", 2)
                    tt(dtm[:, di, 0:n], R3[:, c, s0:s0 + n], ps[bm][:, 0:n], ALU.subtract,
                       reads=[("r3", c, si), ("ps", bm)], writes=[("dt", di)])
                    tt(dtm[:, di, 0:n], dtm[:, di, 0:n], stat[:, 3, 0:n], ALU.mult,
                       reads=[("dt", di), ("stat", 3)], writes=[("dt", di)])
                    act(R2[:, c, s0:s0 + n], dtm[:, di, 0:n], AF.Silu, scale=cc(base + C_LNG + c),
                        bias=cc(base + C_LNB + c), reads=[("dt", di), ("cst",)], writes=[("r2", c, si)])
                    if c == 7:
                        reserved.discard(bm)
                    yield

            for c in range(8):
                i0 = ident[:].unsqueeze(1).to_broadcast([128, 3, 128])
                i1 = cst[:, base + C_CBW + c * 3:base + C_CBW + c * 3 + 3].unsqueeze(2).to_broadcast([128, 3, 128])
                tt(diag[:, 0, c * 3:c * 3 + 3, :], i0, i1, ALU.mult, reads=[("cst",), ("idn",)], writes=[("diag", 0)])
            bgen = pair_stage_gen(l, "w_in", 2 * D, "w_in", 4 * D, AF.Copy, R1, "r1", PAD)
            ln_part1(0)
            for si in range(NSUB):
                next(bgen, None)
                lgen = ln_part2_gen(si)
                next(lgen)
                if si + 1 < NSUB:
                    ln_part1(si + 1)
                nb = 5 if si < 2 else 6
                while True:
                    a_done = next(lgen, "end") == "end"
                    if nb > 0:
                        next(bgen, None)
                        nb -= 1
                    if a_done and nb == 0:
                        break
            for _ in bgen:
                pass
            mask_edges(ti)
            proj_gate(l, "w_a_out", 0, True)
            for half in range(4):
                gw, gres = slab(kview("w_in", l)[:, :, 3 * D + half * 256:3 * D + (half + 1) * 256])
                for m in range(2):
                    c = half * 2 + m
                    for si, (s0, n) in enumerate(SUBS):
                        bv = bank()
                        mm(bv, n, [(diag[:, 0, c * 3 + k, :], R1[:, c, PAD - 1 + s0 + k:PAD - 1 + s0 + k + n])
                                   for k in range(3)],
                           reads=[("diag", 0), ("r1pad",)] + rs("r1", c, ALLS))
                        bg = bank()
                        mm(bg, n, [(gw[:, k, m * 128:(m + 1) * 128], u16[:, k, s0:s0 + n]) for k in range(8)],
                           reads=[gres] + rs("u", ALLC, si))
                        ai = nexti("at", 3)
                        act(at[:, ai, 0:n], ps[bg][:, 0:n], AF.Copy, reads=[("ps", bg)], writes=[("at", ai)])
                        tt(R2[:, c, s0:s0 + n], ps[bv][:, 0:n], at[:, ai, 0:n], ALU.mult,
                           reads=[("ps", bv), ("at", ai)], writes=[("r2", c, si)])
            proj_gate(l, "w_b_out", 1, False)
            for half in range(4):
                qw, qres = slab(kview("w_in", l)[:, :, 5 * D + half * 256:5 * D + (half + 1) * 256])
                for m in range(2):
                    c = half * 2 + m
                    for si, (s0, n) in enumerate(SUBS):
                        b = bank()
                        mm(b, n, [(qw[:, k, m * 128:(m + 1) * 128], u16[:, k, s0:s0 + n]) for k in range(8)],
                           reads=[qres] + rs("u", ALLC, si))
                        act(R1[:, c, PAD + s0:PAD + s0 + n], ps[b][:, 0:n], AF.Copy, reads=[("ps", b)],
                            writes=[("r1", c, si)])
            def attn_qk(si, hd):
                s0, n = SUBS[si]
                bs = []
                for mc in range(2):
                    b = bank()
                    bs.append(b)
                    mm(b, n, [(KT[:, l, 2 * hd + dc, mc * 128:(mc + 1) * 128],
                               R1[:, 2 * hd + dc, PAD + s0:PAD + s0 + n]) for dc in range(2)],
                       reads=[("kv",)] + rs("r1", [2 * hd, 2 * hd + 1], si))
                pi = nexti("pT", 2)
                for mc in range(2):
                    act(pT[:, pi, mc, 0:n], ps[bs[mc]][:, 0:n], AF.Exp, scale=1.0 / 16.0,
                        reads=[("ps", bs[mc])], writes=[("pT", pi, mc)])
                return pi

            def attn_pv(si, hd, pi):
                s0, n = SUBS[si]
                bsum = bank()
                mm(bsum, n, [(ones1[:], pT[:, pi, mc, 0:n]) for mc in range(2)],
                   reads=[("pT", pi, 0), ("pT", pi, 1), ("one",)])
                bo = []
                for dc in range(2):
                    b = bank()
                    bo.append(b)
                    mm(b, n, [(VV[:, l, mc, (2 * hd + dc) * 128:(2 * hd + dc + 1) * 128], pT[:, pi, mc, 0:n])
                              for mc in range(2)], reads=[("kv",), ("pT", pi, 0), ("pT", pi, 1)])
                k1 = nexti("stat", 2) * 2
                k0 = k1 + 1
                act(stat[:, k1, 0:n], ps[bsum][:, 0:n], AF.Ln, reads=[("ps", bsum)], writes=[("stat", k1)])
                act(stat[:, k0, 0:n], stat[:, k1, 0:n], AF.Exp, scale=-1.0, reads=[("stat", k1)],
                    writes=[("stat", k0)])
                for dc in range(2):
                    tt(R2[:, 2 * hd + dc, s0:s0 + n], ps[bo[dc]][:, 0:n], stat[:, k0, 0:n], ALU.mult,
                       reads=[("ps", bo[dc]), ("stat", k0)], writes=[("r2", 2 * hd + dc, si)])

            items = [(si, hd) for si in range(NSUB) for hd in range(4)]
            prev = attn_qk(*items[0])
            for i in range(1, len(items)):
                cur = attn_qk(*items[i])
                attn_pv(*items[i - 1], prev)
                prev = cur
            attn_pv(*items[-1], prev)
            proj_gate(l, "w_x_out", 2, False)
            state["next_norm"] = (base + C_FFN2N, False)
            stats_begin()
            for half in range(4):
                ww, wres = slab(kview("w_o", l)[:, :, half * 256:(half + 1) * 256])
                for m in range(2):
                    c = half * 2 + m
                    for si, (s0, n) in enumerate(SUBS):
                        b = bank()
                        mm(b, n, [(ww[:, k, m * 128:(m + 1) * 128], R3[:, k, s0:s0 + n]) for k in range(8)],
                           reads=[wres] + rs("r3", ALLC, si))
                        tt(x32[:, c, s0:s0 + n], x32[:, c, s0:s0 + n], ps[b][:, 0:n], ALU.add,
                           reads=[("ps", b), ("x", c, si)], writes=[("x", c, si)])
                        x_updated(c, si)

        def diag_prologue(l):
            for c in range(8):
                di = c % 2
                build_diag(di, l * PER + C_CAW + c * 31, 31)
                P.op("sp", lambda s_, c=c, di=di: s_.dma_start(
                    out=dg_d[l, c], in_=diag[:, di, :, :].rearrange("p k j -> p (k j)")),
                    reads=[("diag", di)], writes=[("dgd", di)], dma=f"dgst{di}")

        def kv_prologue(g, first=False):
            P.barrier()
            P.op("sp", lambda s: s.dma_start(out=mem32, in_=mem_d[g].rearrange("(c p) m -> p c m", p=128)),
                 reads=[], writes=[("mem",), ("ost",)], dma="mld")
            n = NMEM
            act(sq[:, :, 0:n], mem32, AF.Square, reads=[("mem",)], writes=SQK)
            b = bank()
            mm(b, n, [(ones_m[:], sq[:, c, 0:n]) for c in range(8)], reads=SQK + [("one",)])
            act(stat[:, 0, 0:n], ps[b][:, 0:n], AF.Ln, bias=cc(C_EPS), reads=[("ps", b), ("cst",)],
                writes=[("stat", 0)])
            act(stat[:, 1, 0:n], stat[:, 0, 0:n], AF.Exp, scale=-0.5, reads=[("stat", 0)], writes=[("stat", 1)])
            for l in range(L):
                for c in range(8):
                    stt(memn[:, c, :], mem32[:, c, :], cc(l * PER + C_MEMN + c), stat[:, 1, 0:n], ALU.mult, ALU.mult,
                        reads=[("mem",), ("stat", 1), ("cst",)], writes=[("memn",)])
                if first:
                    diag_prologue(l)
                for half in range(4):
                    ww, wres = slab(kview("w_kv", l)[:, :, half * 256:(half + 1) * 256])
                    for m in range(2):
                        c = half * 2 + m
                        b = bank()
                        mm(b, n, [(ww[:, k, m * 128:(m + 1) * 128], memn[:, k, :]) for k in range(8)],
                           reads=[wres, ("memn",)])
                        act(KT[:, l, c, :], ps[b][:, 0:n], AF.Copy, reads=[("ps", b)], writes=[("kv",)])
                for half in range(4):
                    ww, wres = slab(kview("w_kv", l)[:, :, D + half * 256:D + (half + 1) * 256])
                    for mc in range(2):
                        b = bank()
                        mm(b, 256, [(memn[:, k, mc * 128:(mc + 1) * 128], ww[:, k, :]) for k in range(8)],
                           reads=[wres, ("memn",)])
                        act(VV[:, l, mc, half * 256:(half + 1) * 256], ps[b][:, 0:256], AF.Copy,
                            reads=[("ps", b)], writes=[("kv",)])

        P.op("sp", lambda s: s.dma_start(out=cst[:], in_=cst_d), reads=[], writes=[("cst",)], dma="cld")
        P.op("sp", lambda s: s.dma_start(out=maskt[:], in_=msk_d), reads=[], writes=[("cst",)], dma="cld")
        P.op("sp", lambda s: s.dma_start(out=id32[:], in_=idn_d), reads=[], writes=[("id32",)], dma="cld")
        P.op("act", lambda a: a.activation(out=ident[:], in_=id32[:], func=AF.Copy), reads=[("id32",), ("cst",)],
             writes=[("idn",)])
        memset(ones_m[:], 1.0 / 1024.0, writes=[("one",)])
        memset(ones1[:], 1.0, writes=[("one",)])

        def load_x(ti):
            for si, (s0, n) in enumerate(SUBS):
                src = xt_d[ti, 128 * 8 * s0:128 * 8 * (s0 + n)].rearrange("(p c n) -> p c n", p=128, c=8)
                P.op("sp", lambda s, src=src, s0=s0, n=n: s.dma_start(out=x32[:, :, s0:s0 + n], in_=src),
                     reads=[], writes=rs("x", ALLC, si), dma=f"xld{si}")

        for g, tiles in groups:
            load_x(tiles[0])
            kv_prologue(g, first=(g == groups[0][0]))
            for idx, ti in enumerate(tiles):
                for l in range(L):
                    ffn(l, "ffn1", l * PER + C_FFN1N)
                    mixer(l, ti)
                    ffn(l, "ffn2", l * PER + C_FFN2N)
                P.barrier()
                rmsnorm(C_FINAL, final=True)
                if idx + 1 < len(tiles):
                    load_x(tiles[idx + 1])
                P.op("sp", lambda s, ti=ti: s.dma_start(out=yt_d[ti].rearrange("p (c t) -> p c t", c=8), in_=ostage),
                     reads=[("ost",)], writes=[], dma="st")

        semnames = sorted(P.cnt.keys())
        sems = {n: es.enter_context(nc.semaphore(n)) for n in semnames}
        block = es.enter_context(nc.Block())

        def emit(engname, eng):
            for waits, fn, semname, inc in P.ops[engname]:
                for s, c in waits:
                    eng.wait_ge(sems[s], c)
                fn(eng).then_inc(sems[semname], inc)
            if engname == "sp":
                eng.wait_ge(sems["st"], P.cnt["st"])

        @block.sync
        def _(e):
            emit("sp", e)

        @block.scalar
        def _(e):
            emit("act", e)

        @block.vector
        def _(e):
            emit("dve", e)

        @block.gpsimd
        def _(e):
            emit("pool", e)

        @block.tensor
        def _(e):
            emit("pe", e)

    return nc


def _pc(v):
    return np.ascontiguousarray(np.asarray(v, np.float32).reshape(8, 128).T)


def prep_inputs(inp):
    f = lambda k: np.asarray(inp[k], np.float32)
    consts = np.zeros((128, NC_), np.float32)
    for l in range(L):
        b = l * PER
        consts[:, b + C_FFN1N:b + C_FFN1N + 8] = _pc(f("ffn1_norm")[l])
        consts[:, b + C_MIXN:b + C_MIXN + 8] = _pc(f("mix_norm")[l])
        consts[:, b + C_MEMN:b + C_MEMN + 8] = _pc(f("mem_norm")[l])
        consts[:, b + C_FFN2N:b + C_FFN2N + 8] = _pc(f("ffn2_norm")[l])
        consts[:, b + C_CAB:b + C_CAB + 8] = _pc(f("conv_a_b")[l])
        consts[:, b + C_LNG:b + C_LNG + 8] = _pc(f("ln_a_g")[l])
        consts[:, b + C_LNB:b + C_LNB + 8] = _pc(f("ln_a_b")[l])
        caw = f("conv_a_w")[l]
        consts[:, b + C_CAW:b + C_CAW + 248] = caw.reshape(31, 8, 128).transpose(2, 1, 0).reshape(128, 248)
        cbw = f("conv_b_w")[l]
        consts[:, b + C_CBW:b + C_CBW + 24] = cbw.reshape(3, 8, 128).transpose(2, 1, 0).reshape(128, 24)
    consts[:, C_FINAL:C_FINAL + 8] = _pc(f("final_norm"))
    consts[:, C_EPS] = EPS
    ident = np.eye(128, dtype=np.float32)
    weights = {name: np.ascontiguousarray(f(name)) for name, _, _ in WNAMES}
    xp = f("x_prompt")[0]
    xs = f("x_sample")
    mp = f("mem_prompt")[0]
    ms = f("mem_sample")
    in_maps = []
    for i in range(8):
        xt = np.zeros((NT, 128 * 8 * W), np.float32)
        masks = np.zeros((128, NT, 60), np.float32)
        for t in range(NT):
            if t < 2:
                seq, start = xp, i * 2048 + t * TOUT - HALO
            else:
                seq, start = xs[i], (t - 2) * TOUT - HALO
            S = seq.shape[0]
            lo, hi = max(start, 0), min(start + W, S)
            xw = np.zeros((D, W), np.float32)
            xw[:, lo - start:hi - start] = seq[lo:hi].T
            xw = xw.reshape(8, 128, W)
            for s0, n in SUBS:
                xt[t, 128 * 8 * s0:128 * 8 * (s0 + n)] = xw[:, :, s0:s0 + n].transpose(1, 0, 2).reshape(-1)
            pos = np.concatenate([np.arange(start, start + 30), np.arange(start + W - 30, start + W)])
            masks[:, t, :] = ((pos >= 0) & (pos < S)).astype(np.float32)[None, :]
        memt = np.stack([mp.T, ms[i].T]).astype(np.float32)
        m = dict(weights)
        m.update({"xt": xt, "memt": np.ascontiguousarray(memt), "consts": consts,
                  "masks": np.ascontiguousarray(masks.reshape(128, NT * 60)), "ident": ident})
        in_maps.append(m)
    return in_maps


FULL_GROUPS = [(0, [0, 1]), (1, [2, 3, 4, 5])]


def kernel(**inputs):
    in_maps = prep_inputs(inputs)
    nc = build_program(FULL_GROUPS)
    res = run_bass_kernel_spmd(nc, in_maps, core_ids=list(range(8)))
    y_prompt = np.zeros((1, 16384, D), np.float32)
    y_sample = np.zeros((8, 4096, D), np.float32)
    for i in range(8):
        yt = np.asarray(res.results[i]["yt"], np.float32).reshape(NT, 128, 8, TOUT)
        yt = yt.transpose(0, 2, 1, 3).reshape(NT, D, TOUT)
        for t in range(NT):
            if t < 2:
                s = i * 2048 + t * TOUT
                y_prompt[0, s:s + TOUT, :] = yt[t].T
            else:
                s = (t - 2) * TOUT
                y_sample[i, s:s + TOUT, :] = yt[t].T
    return (y_prompt, y_sample)
```

```python
import contextlib
import numpy as np
import concourse.bass as bass
import concourse.mybir as mybir
from concourse.bass_utils import run_bass_kernel_spmd

F32 = mybir.dt.float32
BF16 = mybir.dt.bfloat16
AF = mybir.ActivationFunctionType
ALU = mybir.AluOpType

D = 1024
NCH = 8
DFF = 2816
NJ = 22
L = 2
DIN = 9216
NMEM = 256
HALO = 30
TOUT = 1024
W = TOUT + 2 * HALO
SUBS = [(0, 512), (512, 512), (1024, 60)]
NSUB = len(SUBS)
NT = 6
PAD = 15
WP = W + 2 * PAD
RING = 4
NTD = 7
SLAB = 4096
EPS = 1e-6
ALLC = list(range(8))
ALLS = list(range(NSUB))
SQK = [("sq", c) for c in range(8)]

C_FFN1N, C_MIXN, C_MEMN, C_FFN2N, C_CAB, C_LNG, C_LNB, C_CAW, C_CBW = 0, 8, 16, 24, 32, 40, 48, 56, 304
PER = 328
C_FINAL = 2 * PER
C_EPS = C_FINAL + 8
NC_ = 672

SAME_ENGINE_SYNC = False
ENGS = ("pe", "act", "dve", "pool", "sp")

WNAMES = [("ffn1_wg", D, DFF), ("ffn1_wu", D, DFF), ("ffn1_wd", DFF, D), ("w_in", D, DIN),
          ("w_a_out", D, D), ("w_b_out", D, D), ("w_kv", D, 2 * D), ("w_x_out", D, D), ("w_o", D, D),
          ("ffn2_wg", D, DFF), ("ffn2_wu", D, DFF), ("ffn2_wd", DFF, D)]


class Prog:
    def __init__(self):
        self.ops = {e: [] for e in ENGS}
        self.cnt = {}
        self.lastw = {}
        self.readers = {}
        self.waited = {e: {} for e in ENGS}
        self.pending = {e: {} for e in ENGS}
        self.nops = 0
        self.stamp = {}

    def op(self, eng, fn, reads=(), writes=(), dma=None):
        self.nops += 1
        for r in reads:
            if r[0] == "ps":
                self.stamp[r[1]] = self.nops
        for r in writes:
            if r[0] == "ps":
                self.stamp[r[1]] = self.nops
        semname = dma if dma else eng
        inc = 16 if dma else 1
        self.cnt[semname] = self.cnt.get(semname, 0) + inc
        me = (semname, self.cnt[semname])
        deps = dict(self.pending[eng])
        self.pending[eng] = {}

        def add(d):
            if d is None:
                return
            s, c = d
            if s == eng and (eng == "pe" or not SAME_ENGINE_SYNC):
                return
            if deps.get(s, 0) < c:
                deps[s] = c

        for r in reads:
            add(self.lastw.get(r))
        for r in writes:
            add(self.lastw.get(r))
            rd = self.readers.get(r)
            if rd:
                for s, c in rd.items():
                    add((s, c))
        waits = []
        wd = self.waited[eng]
        for s, c in deps.items():
            if wd.get(s, 0) < c:
                wd[s] = c
                waits.append((s, c))
        for r in reads:
            self.readers.setdefault(r, {})[me[0]] = me[1]
        for r in writes:
            self.lastw[r] = me
            self.readers[r] = {}
        self.ops[eng].append((waits, fn, semname, inc))

    def barrier(self):
        names = [n for n in self.cnt if not n.startswith(("ring", "xld", "st", "dgld"))]
        for e in ("pe", "act", "dve", "sp"):
            for n in names:
                if n == e:
                    continue
                c = self.cnt[n]
                if self.pending[e].get(n, 0) < c:
                    self.pending[e][n] = c


def rs(name, cs, ss):
    if isinstance(cs, int):
        cs = [cs]
    if isinstance(ss, int):
        ss = [ss]
    return [(name, c, s) for c in cs for s in ss]


def build_program(groups):
    nc = bass.Bass("TRN2", target_bir_lowering=False)
    P = Prog()
    dr = {}
    for name, a, b in WNAMES:
        dr[name] = nc.dram_tensor(name, [L, a, b], F32, kind="ExternalInput").ap()
    xt_d = nc.dram_tensor("xt", [NT, 128 * 8 * W], F32, kind="ExternalInput").ap()
    mem_d = nc.dram_tensor("memt", [2, D, NMEM], F32, kind="ExternalInput").ap()
    cst_d = nc.dram_tensor("consts", [128, NC_], F32, kind="ExternalInput").ap()
    msk_d = nc.dram_tensor("masks", [128, NT * 60], F32, kind="ExternalInput").ap()
    idn_d = nc.dram_tensor("ident", [128, 128], F32, kind="ExternalInput").ap()
    yt_d = nc.dram_tensor("yt", [NT, 128, 8 * TOUT], F32, kind="ExternalOutput").ap()
    dg_d = nc.dram_tensor("dgscr", [L, 8, 128, 31 * 128], BF16, kind="Internal").ap()

    es = contextlib.ExitStack()

    def T(name, shape, dt):
        return es.enter_context(nc.sbuf_tensor(name, shape, dt))

    with es:
        x32 = T("x32", [128, 8, W], F32)
        u16 = T("u16", [128, 8, W], BF16)
        SCR = 8 * WP + 16 * W
        scr = T("scr", [128, SCR], BF16)
        ring = T("ring", [128, RING, SLAB], BF16)
        KT = T("KT", [128, L, 8, NMEM], BF16)
        VV = T("VV", [128, L, 2, D], BF16)
        diag = T("diag", [128, 2, 31, 128], BF16)
        at = T("at", [128, 3, 512], F32)
        dtm = T("dtm", [128, 2, 512], F32)
        stat = T("stat", [128, 4, 512], F32)
        stat2 = T("stat2", [128, 2, 64], F32)
        pT = T("pT", [128, 2, 2, 512], BF16)
        sq = T("sq", [128, 8, 512], BF16)
        cst = T("cst", [128, NC_], F32)
        maskt = T("maskt", [128, NT * 60], F32)
        id32 = T("id32", [128, 128], F32)
        ident = T("identb", [128, 128], BF16)
        ones_m = T("ones_m", [128, 128], BF16)
        ones1 = T("ones1", [128, 128], BF16)
        ps = [es.enter_context(nc.psum_tensor(f"ps{b}", [128, 512], F32)) for b in range(8)]

        R1 = scr[:, 0:8 * WP].rearrange("p (c w) -> p c w", c=8)
        R2 = scr[:, 8 * WP:8 * WP + 8 * W].rearrange("p (c w) -> p c w", c=8)
        R3 = scr[:, 8 * WP + 8 * W:8 * WP + 16 * W].rearrange("p (c w) -> p c w", c=8)
        hbuf = scr[:, 0:NJ * W].rearrange("p (c w) -> p c w", c=NJ)
        ostage = scr[:, 0:2 * 8 * TOUT].bitcast(F32).rearrange("p (c w) -> p c w", c=8)
        mem32 = scr[:, 0:2 * 8 * NMEM].bitcast(F32).rearrange("p (c w) -> p c w", c=8)
        memn = scr[:, 2 * 8 * NMEM:3 * 8 * NMEM].rearrange("p (c w) -> p c w", c=8)

        state = {"bank": 0, "slab": 0}
        rot = {}

        def nexti(name, n):
            i = rot.get(name, 0)
            rot[name] = (i + 1) % n
            return i

        reserved = set()

        def bank():
            free = [b for b in range(8) if b not in reserved]
            b = min(free, key=lambda k: (P.stamp.get(k, 0), k))
            P.stamp[b] = P.nops + 1
            return b

        def stats_begin():
            sb = [bank() for _ in range(NSUB)]
            reserved.update(sb)
            state["statb"] = sb

        def x_updated(c, si):
            sb = state.get("statb")
            if sb is None:
                return
            s0, n = SUBS[si]
            b = sb[si]
            q = nexti("sqslot", 8)
            act(sq[:, q, 0:n], x32[:, c, s0:s0 + n], AF.Square, reads=[("x", c, si)], writes=[("sq", q)])
            pend = state.setdefault("pend", [])
            pend.append(lambda: P.op(
                "pe", lambda t: t.matmul(ps[b][:, 0:n], ones_m[:], sq[:, q, 0:n], start=(c == 0), stop=(c == 7)),
                reads=[("sq", q), ("one",)], writes=[("ps", b)]))
            while len(pend) > 4:
                pend.pop(0)()

        def mm(b, n, pairs, reads, usplit=None):
            np_ = len(pairs)
            if usplit is not None and state.pop("split_next", False):
                for i, (l, r) in enumerate(pairs):
                    P.op("pe", lambda t, l=l, r=r, i=i: t.matmul(ps[b][:, 0:n], l, r, start=(i == 0),
                                                               stop=(i == np_ - 1)),
                         reads=[x for x in reads if x[0] != "u"] + [("u", i, usplit)], writes=[("ps", b)])
                return

            def fn(t):
                last = None
                for i, (l, r) in enumerate(pairs):
                    last = t.matmul(ps[b][:, 0:n], l, r, start=(i == 0), stop=(i == np_ - 1))
                return last
            P.op("pe", fn, reads=reads, writes=[("ps", b)])

        def act(out, in_, func, reads, writes, bias=None, scale=None):
            kw = {}
            if bias is not None:
                kw["bias"] = bias
            if scale is not None:
                kw["scale"] = scale
            P.op("act", lambda a: a.activation(out=out, in_=in_, func=func, **kw), reads=reads, writes=writes)

        def tt(out, in0, in1, op, reads, writes):
            P.op("dve", lambda v: v.tensor_tensor(out=out, in0=in0, in1=in1, op=op), reads=reads, writes=writes)

        def stt(out, in0, scalar, in1, op0, op1, reads, writes):
            P.op("dve", lambda v: v.scalar_tensor_tensor(out=out, in0=in0, scalar=scalar, in1=in1, op0=op0, op1=op1),
                 reads=reads, writes=writes)

        def memset(ap, val, writes):
            P.op("dve", lambda v: v.memset(ap, val), reads=[], writes=writes)

        def slab(src):
            i = state["slab"]
            state["slab"] += 1
            slot = i % RING
            _, a, b = src.shape
            assert a * b <= SLAB
            dst = ring[:, slot, 0:a * b].rearrange("p (a b) -> p a b", a=a)
            P.op("pool", lambda g: g.dma_start(out=dst, in_=src), reads=[], writes=[("slot", slot)], dma=f"ring{slot}")
            return dst, ("slot", slot)

        def kview(name, l):
            return dr[name][l].rearrange("(k p) n -> p k n", p=128)

        def cc(col):
            return cst[:, col:col + 1]

        def rmsnorm(goff, final=False):
            sb = state.pop("statb", None)
            if sb is not None:
                reserved.difference_update(sb)
                for f in state.pop("pend", []):
                    f()
            for si, (s0, n) in enumerate(SUBS):
                if sb is not None:
                    b = sb[si]
                else:
                    act(sq[:, :, 0:n], x32[:, :, s0:s0 + n], AF.Square, reads=rs("x", ALLC, si), writes=SQK)
                    b = bank()
                    mm(b, n, [(ones_m[:], sq[:, c, 0:n]) for c in range(8)], reads=SQK + [("one",)])
                if n <= 64:
                    sbuf_, skey, k0 = stat2, "stat2", 0
                else:
                    sbuf_, skey, k0 = stat, "stat", (si % 2) * 2
                act(sbuf_[:, k0, 0:n], ps[b][:, 0:n], AF.Ln, bias=cc(C_EPS), reads=[("ps", b), ("cst",)],
                    writes=[(skey, k0)])
                act(sbuf_[:, k0 + 1, 0:n], sbuf_[:, k0, 0:n], AF.Exp, scale=-0.5, reads=[(skey, k0)],
                    writes=[(skey, k0 + 1)])
                for c in range(8):
                    if not final:
                        stt(u16[:, c, s0:s0 + n], x32[:, c, s0:s0 + n], cc(goff + c), sbuf_[:, k0 + 1, 0:n],
                            ALU.mult, ALU.mult, reads=[("x", c, si), (skey, k0 + 1), ("cst",)],
                            writes=[("u", c, si)])
                    else:
                        lo = max(s0, HALO)
                        hi = min(s0 + n, HALO + TOUT)
                        stt(ostage[:, c, lo - HALO:hi - HALO], x32[:, c, lo:hi], cc(goff + c),
                            sbuf_[:, k0 + 1, lo - s0:hi - s0], ALU.mult, ALU.mult,
                            reads=[("x", c, si), (skey, k0 + 1), ("cst",)], writes=[("ost",)])

        def pair_stage(*a, **kw):
            for _ in pair_stage_gen(*a, **kw):
                pass

        def pair_stage_gen(l, pname, pcol, qname, qcol, func, dst, dname, doff, halves=(0, 1, 2, 3)):
            for half in halves:
                pw, pres = slab(kview(pname, l)[:, :, pcol + half * 256:pcol + (half + 1) * 256])
                qw, qres = slab(kview(qname, l)[:, :, qcol + half * 256:qcol + (half + 1) * 256])
                for m in range(2):
                    c = half * 2 + m
                    for si, (s0, n) in enumerate(SUBS):
                        bp = bank()
                        mm(bp, n, [(pw[:, k, m * 128:(m + 1) * 128], u16[:, k, s0:s0 + n]) for k in range(8)],
                           reads=[pres] + rs("u", ALLC, si), usplit=si)
                        bq = bank()
                        mm(bq, n, [(qw[:, k, m * 128:(m + 1) * 128], u16[:, k, s0:s0 + n]) for k in range(8)],
                           reads=[qres] + rs("u", ALLC, si))
                        ai = nexti("at", 3)
                        act(at[:, ai, 0:n], ps[bq][:, 0:n], func, reads=[("ps", bq)], writes=[("at", ai)])
                        tt(dst[:, c, doff + s0:doff + s0 + n], ps[bp][:, 0:n], at[:, ai, 0:n], ALU.mult,
                           reads=[("ps", bp), ("at", ai)], writes=[(dname, c, si)])
                        yield

        def ffn(l, pre, goff):
            P.barrier()
            rmsnorm(goff)
            state["split_next"] = True
            wg, wu, wd = pre + "_wg", pre + "_wu", pre + "_wd"
            for jj in range(NJ // 2):
                gw, gres = slab(kview(wg, l)[:, :, jj * 256:(jj + 1) * 256])
                uw, ures = slab(kview(wu, l)[:, :, jj * 256:(jj + 1) * 256])
                for m in range(2):
                    j = jj * 2 + m
                    for si, (s0, n) in enumerate(SUBS):
                        bg = bank()
                        mm(bg, n, [(gw[:, k, m * 128:(m + 1) * 128], u16[:, k, s0:s0 + n]) for k in range(8)],
                           reads=[gres] + rs("u", ALLC, si), usplit=si)
                        bu = bank()
                        mm(bu, n, [(uw[:, k, m * 128:(m + 1) * 128], u16[:, k, s0:s0 + n]) for k in range(8)],
                           reads=[ures] + rs("u", ALLC, si))
                        ai = nexti("at", 3)
                        act(at[:, ai, 0:n], ps[bg][:, 0:n], AF.Silu, reads=[("ps", bg)], writes=[("at", ai)])
                        tt(hbuf[:, j, s0:s0 + n], ps[bu][:, 0:n], at[:, ai, 0:n], ALU.mult,
                           reads=[("ps", bu), ("at", ai)], writes=[("h", j, si), ("ost",)])
            wdv = dr[wd][l].rearrange("(j p) n -> p j n", p=128)
            stats_begin()
            for m in range(8):
                dw, dres = slab(wdv[:, :, m * 128:(m + 1) * 128])
                for si, (s0, n) in enumerate(SUBS):
                    b = bank()
                    mm(b, n, [(dw[:, j, :], hbuf[:, j, s0:s0 + n]) for j in range(NJ)],
                       reads=[dres] + rs("h", list(range(NJ)), si))
                    stt(x32[:, m, s0:s0 + n], ps[b][:, 0:n], 0.5, x32[:, m, s0:s0 + n], ALU.mult, ALU.add,
                        reads=[("ps", b), ("x", m, si)], writes=[("x", m, si)])
                    x_updated(m, si)

        def zero_pads():
            memset(R1[:, :, 0:PAD], 0.0, writes=[("r1pad",)])
            memset(R1[:, :, PAD + W:WP], 0.0, writes=[("r1pad",)])

        def mask_edges(ti):
            ml = maskt[:, ti * 60:ti * 60 + 30].unsqueeze(1).to_broadcast([128, 8, 30])
            mr = maskt[:, ti * 60 + 30:ti * 60 + 60].unsqueeze(1).to_broadcast([128, 8, 30])
            tt(R1[:, :, PAD:PAD + 30], R1[:, :, PAD:PAD + 30], ml, ALU.mult,
               reads=rs("r1", ALLC, 0) + [("cst",)], writes=rs("r1", ALLC, 0))
            tt(R1[:, :, PAD + W - 30:PAD + W], R1[:, :, PAD + W - 30:PAD + W], mr, ALU.mult,
               reads=rs("r1", ALLC, 2) + [("cst",)], writes=rs("r1", ALLC, 2))

        def proj_gate(l, wname, gi, first):
            gcol = 6 * D + gi * D
            for half in range(4):
                ww, wres = slab(kview(wname, l)[:, :, half * 256:(half + 1) * 256])
                gw, gres = slab(kview("w_in", l)[:, :, gcol + half * 256:gcol + (half + 1) * 256])
                for m in range(2):
                    c = half * 2 + m
                    for si, (s0, n) in enumerate(SUBS):
                        by = bank()
                        mm(by, n, [(ww[:, k, m * 128:(m + 1) * 128], R2[:, k, s0:s0 + n]) for k in range(8)],
                           reads=[wres] + rs("r2", ALLC, si))
                        bg = bank()
                        mm(bg, n, [(gw[:, k, m * 128:(m + 1) * 128], u16[:, k, s0:s0 + n]) for k in range(8)],
                           reads=[gres] + rs("u", ALLC, si))
                        ai = nexti("at", 3)
                        act(at[:, ai, 0:n], ps[bg][:, 0:n], AF.Sigmoid, reads=[("ps", bg)], writes=[("at", ai)])
                        if first:
                            tt(R3[:, c, s0:s0 + n], ps[by][:, 0:n], at[:, ai, 0:n], ALU.mult,
                               reads=[("ps", by), ("at", ai)], writes=[("r3", c, si)])
                        else:
                            di = nexti("dt", 2)
                            tt(dtm[:, di, 0:n], ps[by][:, 0:n], at[:, ai, 0:n], ALU.mult,
                               reads=[("ps", by), ("at", ai)], writes=[("dt", di)])
                            tt(R3[:, c, s0:s0 + n], R3[:, c, s0:s0 + n], dtm[:, di, 0:n], ALU.add,
                               reads=[("r3", c, si), ("dt", di)], writes=[("r3", c, si)])

        def build_diag(di, col0, ntap):
            i0 = ident[:].unsqueeze(1).to_broadcast([128, ntap, 128])
            i1 = cst[:, col0:col0 + ntap].unsqueeze(2).to_broadcast([128, ntap, 128])
            tt(diag[:, di, 0:ntap, :], i0, i1, ALU.mult, reads=[("cst",), ("idn",)], writes=[("diag", di)])

        def load_diag(l, c):
            di = c % 2
            P.op("sp", lambda s_: s_.dma_start(out=diag[:, di, :, :].rearrange("p k j -> p (k j)"), in_=dg_d[l, c]),
                 reads=[("dgd", 0), ("dgd", 1)], writes=[("diag", di)], dma=f"dgld{di}")

        def mixer(l, ti):
            base = l * PER
            P.barrier()
            rmsnorm(base + C_MIXN)
            state["split_next"] = True
            zero_pads()
            pair_stage(l, "w_in", 0, "w_in", D, AF.Sigmoid, R1, "r1", PAD)
            mask_edges(ti)
            dis = [c % 2 for c in range(8)]
            load_diag(l, 0)
            for c in range(8):
                di = dis[c]
                if c + 1 < 8:
                    load_diag(l, c + 1)
                for si, (s0, n) in enumerate(SUBS):
                    b = bank()
                    mm(b, n, [(diag[:, di, k, :], R1[:, c, s0 + k:s0 + k + n]) for k in range(NTD, 31)],
                       reads=[("diag", di), ("r1pad",)] + rs("r1", c, ALLS))
                    ci = nexti("dt", 2)
                    act(dtm[:, ci, 0:n], ps[b][:, 0:n], AF.Identity, bias=cc(base + C_CAB + c),
                        reads=[("ps", b), ("cst",)], writes=[("dt", ci)])
                    for k in range(NTD):
                        last = (k == NTD - 1)
                        stt(R3[:, c, s0:s0 + n] if last else dtm[:, ci, 0:n], R1[:, c, s0 + k:s0 + k + n],
                            cc(base + C_CAW + c * 31 + k), dtm[:, ci, 0:n], ALU.mult, ALU.add,
                            reads=[("dt", ci), ("cst",), ("r1pad",)] + rs("r1", c, ALLS),
                            writes=[("r3", c, si)] if last else [("dt", ci)])
            def ln_part1(si):
                s0, n = SUBS[si]
                act(sq[:, :, 0:n], R3[:, :, s0:s0 + n], AF.Square, reads=rs("r3", ALLC, si), writes=SQK)

            def ln_part2_gen(si):
                s0, n = SUBS[si]
                bm = bank()
                reserved.add(bm)
                mm(bm, n, [(ones_m[:], R3[:, c, s0:s0 + n]) for c in range(8)], reads=rs("r3", ALLC, si) + [("one",)])
                bq = bank()
                reserved.add(bq)
                mm(bq, n, [(ones_m[:], sq[:, c, 0:n]) for c in range(8)], reads=SQK + [("one",)])
                act(stat[:, 0, 0:n], ps[bm][:, 0:n], AF.Square, reads=[("ps", bm)], writes=[("stat", 0)])
                tt(stat[:, 1, 0:n], ps[bq][:, 0:n], stat[:, 0, 0:n], ALU.subtract,
                   reads=[("ps", bq), ("stat", 0)], writes=[("stat", 1)])
                act(stat[:, 2, 0:n], stat[:, 1, 0:n], AF.Ln, bias=cc(C_EPS), reads=[("stat", 1), ("cst",)],
                    writes=[("stat", 2)])
                act(stat[:, 3, 0:n], stat[:, 2, 0:n], AF.Exp, scale=-0.5, reads=[("stat", 2)], writes=[("stat", 3)])
                reserved.discard(bq)
                yield
                for c in range(8):
                    di = nexti("dt", 2)
                    tt(dtm[:, di, 0:n], R3[:, c, s0:s0 + n], ps[bm][:, 0:n], ALU.subtract,
                       reads=[("r3", c, si), ("ps", bm)], writes=[("dt", di)])
                    tt(dtm[:, di, 0:n], dtm[:, di, 0:n], stat[:, 3, 0:n], ALU.mult,
                       reads=[("dt", di), ("stat", 3)], writes=[("dt", di)])
                    act(R2[:, c, s0:s0 + n], dtm[:, di, 0:n], AF.Silu, scale=cc(base + C_LNG + c),
                        bias=cc(base + C_LNB + c), reads=[("dt", di), ("cst",)], writes=[("r2", c, si)])
                    if c == 7:
                        reserved.discard(bm)
                    yield

            for c in range(8):
                i0 = ident[:].unsqueeze(1).to_broadcast([128, 3, 128])
                i1 = cst[:, base + C_CBW + c * 3:base + C_CBW + c * 3 + 3].unsqueeze(2).to_broadcast([128, 3, 128])
                tt(diag[:, 0, c * 3:c * 3 + 3, :], i0, i1, ALU.mult, reads=[("cst",), ("idn",)], writes=[("diag", 0)])
            bgen = pair_stage_gen(l, "w_in", 2 * D, "w_in", 4 * D, AF.Copy, R1, "r1", PAD)
            ln_part1(0)
            for si in range(NSUB):
                next(bgen, None)
                lgen = ln_part2_gen(si)
                next(lgen)
                if si + 1 < NSUB:
                    ln_part1(si + 1)
                nb = 5 if si < 2 else 6
                while True:
                    a_done = next(lgen, "end") == "end"
                    if nb > 0:
                        next(bgen, None)
                        nb -= 1
                    if a_done and nb == 0:
                        break
            for _ in bgen:
                pass
            mask_edges(ti)
            proj_gate(l, "w_a_out", 0, True)
            for half in range(4):
                gw, gres = slab(kview("w_in", l)[:, :, 3 * D + half * 256:3 * D + (half + 1) * 256])
                for m in range(2):
                    c = half * 2 + m
                    for si, (s0, n) in enumerate(SUBS):
                        bv = bank()
                        mm(bv, n, [(diag[:, 0, c * 3 + k, :], R1[:, c, PAD - 1 + s0 + k:PAD - 1 + s0 + k + n])
                                   for k in range(3)],
                           reads=[("diag", 0), ("r1pad",)] + rs("r1", c, ALLS))
                        bg = bank()
                        mm(bg, n, [(gw[:, k, m * 128:(m + 1) * 128], u16[:, k, s0:s0 + n]) for k in range(8)],
                           reads=[gres] + rs("u", ALLC, si))
                        ai = nexti("at", 3)
                        act(at[:, ai, 0:n], ps[bg][:, 0:n], AF.Copy, reads=[("ps", bg)], writes=[("at", ai)])
                        tt(R2[:, c, s0:s0 + n], ps[bv][:, 0:n], at[:, ai, 0:n], ALU.mult,
                           reads=[("ps", bv), ("at", ai)], writes=[("r2", c, si)])
            proj_gate(l, "w_b_out", 1, False)
            for half in range(4):
                qw, qres = slab(kview("w_in", l)[:, :, 5 * D + half * 256:5 * D + (half + 1) * 256])
                for m in range(2):
                    c = half * 2 + m
                    for si, (s0, n) in enumerate(SUBS):
                        b = bank()
                        mm(b, n, [(qw[:, k, m * 128:(m + 1) * 128], u16[:, k, s0:s0 + n]) for k in range(8)],
                           reads=[qres] + rs("u", ALLC, si))
                        act(R1[:, c, PAD + s0:PAD + s0 + n], ps[b][:, 0:n], AF.Copy, reads=[("ps", b)],
                            writes=[("r1", c, si)])
            def attn_qk(si, hd):
                s0, n = SUBS[si]
                bs = []
                for mc in range(2):
                    b = bank()
                    bs.append(b)
                    mm(b, n, [(KT[:, l, 2 * hd + dc, mc * 128:(mc + 1) * 128],
                               R1[:, 2 * hd + dc, PAD + s0:PAD + s0 + n]) for dc in range(2)],
                       reads=[("kv",)] + rs("r1", [2 * hd, 2 * hd + 1], si))
                pi = nexti("pT", 2)
                for mc in range(2):
                    act(pT[:, pi, mc, 0:n], ps[bs[mc]][:, 0:n], AF.Exp, scale=1.0 / 16.0,
                        reads=[("ps", bs[mc])], writes=[("pT", pi, mc)])
                return pi

            def attn_pv(si, hd, pi):
                s0, n = SUBS[si]
                bsum = bank()
                mm(bsum, n, [(ones1[:], pT[:, pi, mc, 0:n]) for mc in range(2)],
                   reads=[("pT", pi, 0), ("pT", pi, 1), ("one",)])
                bo = []
                for dc in range(2):
                    b = bank()
                    bo.append(b)
                    mm(b, n, [(VV[:, l, mc, (2 * hd + dc) * 128:(2 * hd + dc + 1) * 128], pT[:, pi, mc, 0:n])
                              for mc in range(2)], reads=[("kv",), ("pT", pi, 0), ("pT", pi, 1)])
                k1 = nexti("stat", 2) * 2
                k0 = k1 + 1
                act(stat[:, k1, 0:n], ps[bsum][:, 0:n], AF.Ln, reads=[("ps", bsum)], writes=[("stat", k1)])
                act(stat[:, k0, 0:n], stat[:, k1, 0:n], AF.Exp, scale=-1.0, reads=[("stat", k1)],
                    writes=[("stat", k0)])
                for dc in range(2):
                    tt(R2[:, 2 * hd + dc, s0:s0 + n], ps[bo[dc]][:, 0:n], stat[:, k0, 0:n], ALU.mult,
                       reads=[("ps", bo[dc]), ("stat", k0)], writes=[("r2", 2 * hd + dc, si)])

            items = [(si, hd) for si in range(NSUB) for hd in range(4)]
            prev = attn_qk(*items[0])
            for i in range(1, len(items)):
                cur = attn_qk(*items[i])
                attn_pv(*items[i - 1], prev)
                prev = cur
            attn_pv(*items[-1], prev)
            proj_gate(l, "w_x_out", 2, False)
            stats_begin()
            for half in range(4):
                ww, wres = slab(kview("w_o", l)[:, :, half * 256:(half + 1) * 256])
                for m in range(2):
                    c = half * 2 + m
                    for si, (s0, n) in enumerate(SUBS):
                        b = bank()
                        mm(b, n, [(ww[:, k, m * 128:(m + 1) * 128], R3[:, k, s0:s0 + n]) for k in range(8)],
                           reads=[wres] + rs("r3", ALLC, si))
                        tt(x32[:, c, s0:s0 + n], x32[:, c, s0:s0 + n], ps[b][:, 0:n], ALU.add,
                           reads=[("ps", b), ("x", c, si)], writes=[("x", c, si)])
                        x_updated(c, si)

        def diag_prologue(l):
            for c in range(8):
                di = c % 2
                build_diag(di, l * PER + C_CAW + c * 31, 31)
                P.op("sp", lambda s_, c=c, di=di: s_.dma_start(
                    out=dg_d[l, c], in_=diag[:, di, :, :].rearrange("p k j -> p (k j)")),
                    reads=[("diag", di)], writes=[("dgd", di)], dma=f"dgst{di}")

        def kv_prologue(g, first=False):
            P.barrier()
            P.op("sp", lambda s: s.dma_start(out=mem32, in_=mem_d[g].rearrange("(c p) m -> p c m", p=128)),
                 reads=[], writes=[("mem",), ("ost",)], dma="mld")
            n = NMEM
            act(sq[:, :, 0:n], mem32, AF.Square, reads=[("mem",)], writes=SQK)
            b = bank()
            mm(b, n, [(ones_m[:], sq[:, c, 0:n]) for c in range(8)], reads=SQK + [("one",)])
            act(stat[:, 0, 0:n], ps[b][:, 0:n], AF.Ln, bias=cc(C_EPS), reads=[("ps", b), ("cst",)],
                writes=[("stat", 0)])
            act(stat[:, 1, 0:n], stat[:, 0, 0:n], AF.Exp, scale=-0.5, reads=[("stat", 0)], writes=[("stat", 1)])
            for l in range(L):
                for c in range(8):
                    stt(memn[:, c, :], mem32[:, c, :], cc(l * PER + C_MEMN + c), stat[:, 1, 0:n], ALU.mult, ALU.mult,
                        reads=[("mem",), ("stat", 1), ("cst",)], writes=[("memn",)])
                if first:
                    diag_prologue(l)
                for half in range(4):
                    ww, wres = slab(kview("w_kv", l)[:, :, half * 256:(half + 1) * 256])
                    for m in range(2):
                        c = half * 2 + m
                        b = bank()
                        mm(b, n, [(ww[:, k, m * 128:(m + 1) * 128], memn[:, k, :]) for k in range(8)],
                           reads=[wres, ("memn",)])
                        act(KT[:, l, c, :], ps[b][:, 0:n], AF.Copy, reads=[("ps", b)], writes=[("kv",)])
                for half in range(4):
                    ww, wres = slab(kview("w_kv", l)[:, :, D + half * 256:D + (half + 1) * 256])
                    for mc in range(2):
                        b = bank()
                        mm(b, 256, [(memn[:, k, mc * 128:(mc + 1) * 128], ww[:, k, :]) for k in range(8)],
                           reads=[wres, ("memn",)])
                        act(VV[:, l, mc, half * 256:(half + 1) * 256], ps[b][:, 0:256], AF.Copy,
                            reads=[("ps", b)], writes=[("kv",)])

        P.op("sp", lambda s: s.dma_start(out=cst[:], in_=cst_d), reads=[], writes=[("cst",)], dma="cld")
        P.op("sp", lambda s: s.dma_start(out=maskt[:], in_=msk_d), reads=[], writes=[("cst",)], dma="cld")
        P.op("sp", lambda s: s.dma_start(out=id32[:], in_=idn_d), reads=[], writes=[("id32",)], dma="cld")
        P.op("act", lambda a: a.activation(out=ident[:], in_=id32[:], func=AF.Copy), reads=[("id32",), ("cst",)],
             writes=[("idn",)])
        memset(ones_m[:], 1.0 / 1024.0, writes=[("one",)])
        memset(ones1[:], 1.0, writes=[("one",)])

        def load_x(ti):
            for si, (s0, n) in enumerate(SUBS):
                src = xt_d[ti, 128 * 8 * s0:128 * 8 * (s0 + n)].rearrange("(p c n) -> p c n", p=128, c=8)
                P.op("sp", lambda s, src=src, s0=s0, n=n: s.dma_start(out=x32[:, :, s0:s0 + n], in_=src),
                     reads=[], writes=rs("x", ALLC, si), dma=f"xld{si}")

        for g, tiles in groups:
            load_x(tiles[0])
            kv_prologue(g, first=(g == groups[0][0]))
            for idx, ti in enumerate(tiles):
                for l in range(L):
                    ffn(l, "ffn1", l * PER + C_FFN1N)
                    mixer(l, ti)
                    ffn(l, "ffn2", l * PER + C_FFN2N)
                P.barrier()
                rmsnorm(C_FINAL, final=True)
                if idx + 1 < len(tiles):
                    load_x(tiles[idx + 1])
                P.op("sp", lambda s, ti=ti: s.dma_start(out=yt_d[ti].rearrange("p (c t) -> p c t", c=8), in_=ostage),
                     reads=[("ost",)], writes=[], dma="st")

        semnames = sorted(P.cnt.keys())
        sems = {n: es.enter_context(nc.semaphore(n)) for n in semnames}
        block = es.enter_context(nc.Block())

        def emit(engname, eng):
            for waits, fn, semname, inc in P.ops[engname]:
                for s, c in waits:
                    eng.wait_ge(sems[s], c)
                fn(eng).then_inc(sems[semname], inc)
            if engname == "sp":
                eng.wait_ge(sems["st"], P.cnt["st"])

        @block.sync
        def _(e):
            emit("sp", e)

        @block.scalar
        def _(e):
            emit("act", e)

        @block.vector
        def _(e):
            emit("dve", e)

        @block.gpsimd
        def _(e):
            emit("pool", e)

        @block.tensor
        def _(e):
            emit("pe", e)

    return nc


def _pc(v):
    return np.ascontiguousarray(np.asarray(v, np.float32).reshape(8, 128).T)


def prep_inputs(inp):
    f = lambda k: np.asarray(inp[k], np.float32)
    consts = np.zeros((128, NC_), np.float32)
    for l in range(L):
        b = l * PER
        consts[:, b + C_FFN1N:b + C_FFN1N + 8] = _pc(f("ffn1_norm")[l])
        consts[:, b + C_MIXN:b + C_MIXN + 8] = _pc(f("mix_norm")[l])
        consts[:, b + C_MEMN:b + C_MEMN + 8] = _pc(f("mem_norm")[l])
        consts[:, b + C_FFN2N:b + C_FFN2N + 8] = _pc(f("ffn2_norm")[l])
        consts[:, b + C_CAB:b + C_CAB + 8] = _pc(f("conv_a_b")[l])
        consts[:, b + C_LNG:b + C_LNG + 8] = _pc(f("ln_a_g")[l])
        consts[:, b + C_LNB:b + C_LNB + 8] = _pc(f("ln_a_b")[l])
        caw = f("conv_a_w")[l]
        consts[:, b + C_CAW:b + C_CAW + 248] = caw.reshape(31, 8, 128).transpose(2, 1, 0).reshape(128, 248)
        cbw = f("conv_b_w")[l]
        consts[:, b + C_CBW:b + C_CBW + 24] = cbw.reshape(3, 8, 128).transpose(2, 1, 0).reshape(128, 24)
    consts[:, C_FINAL:C_FINAL + 8] = _pc(f("final_norm"))
    consts[:, C_EPS] = EPS
    ident = np.eye(128, dtype=np.float32)
    weights = {name: np.ascontiguousarray(f(name)) for name, _, _ in WNAMES}
    xp = f("x_prompt")[0]
    xs = f("x_sample")
    mp = f("mem_prompt")[0]
    ms = f("mem_sample")
    in_maps = []
    for i in range(8):
        xt = np.zeros((NT, 128 * 8 * W), np.float32)
        masks = np.zeros((128, NT, 60), np.float32)
        for t in range(NT):
            if t < 2:
                seq, start = xp, i * 2048 + t * TOUT - HALO
            else:
                seq, start = xs[i], (t - 2) * TOUT - HALO
            S = seq.shape[0]
            lo, hi = max(start, 0), min(start + W, S)
            xw = np.zeros((D, W), np.float32)
            xw[:, lo - start:hi - start] = seq[lo:hi].T
            xw = xw.reshape(8, 128, W)
            for s0, n in SUBS:
                xt[t, 128 * 8 * s0:128 * 8 * (s0 + n)] = xw[:, :, s0:s0 + n].transpose(1, 0, 2).reshape(-1)
            pos = np.concatenate([np.arange(start, start + 30), np.arange(start + W - 30, start + W)])
            masks[:, t, :] = ((pos >= 0) & (pos < S)).astype(np.float32)[None, :]
        memt = np.stack([mp.T, ms[i].T]).astype(np.float32)
        m = dict(weights)
        m.update({"xt": xt, "memt": np.ascontiguousarray(memt), "consts": consts,
                  "masks": np.ascontiguousarray(masks.reshape(128, NT * 60)), "ident": ident})
        in_maps.append(m)
    return in_maps


FULL_GROUPS = [(0, [0, 1]), (1, [2, 3, 4, 5])]


def kernel(**inputs):
    in_maps = prep_inputs(inputs)
    nc = build_program(FULL_GROUPS)
    res = run_bass_kernel_spmd(nc, in_maps, core_ids=list(range(8)))
    y_prompt = np.zeros((1, 16384, D), np.float32)
    y_sample = np.zeros((8, 4096, D), np.float32)
    for i in range(8):
        yt = np.asarray(res.results[i]["yt"], np.float32).reshape(NT, 128, 8, TOUT)
        yt = yt.transpose(0, 2, 1, 3).reshape(NT, D, TOUT)
        for t in range(NT):
            if t < 2:
                s = i * 2048 + t * TOUT
                y_prompt[0, s:s + TOUT, :] = yt[t].T
            else:
                s = (t - 2) * TOUT
                y_sample[i, s:s + TOUT, :] = yt[t].T
    return (y_prompt, y_sample)
```
